# Optimizing a Trainium2 kernel written in Bass

```python
import jax, jax.numpy as jnp
from jax import lax
import numpy as np

D_MODEL = 2048
BATCH = 4
SEQ = 8192
DEPTH = 1

GRID_W = 64
D_ATTN = D_MODEL // 2
HEAD_DIM = 128
N_ATTN_HEADS = D_ATTN // HEAD_DIM
WIN_R = 8
WIN_C = 16
D_REC = D_MODEL - D_ATTN
REC_BLOCKS = 8
REC_BLOCK_W = D_REC // REC_BLOCKS
CONV_W = 4
C_RG = 8.0
D_IN_PROJ = 3 * D_ATTN + 2 * D_REC
D_FF = -(-8 * D_MODEL // (3 * 256)) * 256
D_PLE = 256
EPS = 1e-6
NEG_INF = -1e9

kernel_name = "hymba_natten_rglru_sandwich_encoder"


def rms_norm(x, g):
    x32 = x.astype(jnp.float32)
    y = x32 * lax.rsqrt(jnp.mean(x32 * x32, axis=-1, keepdims=True) + EPS)
    return (y * g.astype(jnp.float32)).astype(x.dtype)


def neighborhood_attention(q, k, v, rpb):
    B, S, H, Dh = q.shape
    rows = S // GRID_W
    kr = min(WIN_R, rows)
    cols = jnp.arange(GRID_W)
    cstart = jnp.clip(cols - WIN_C // 2, 0, GRID_W - WIN_C)
    col_ok = (cols[None, :] >= cstart[:, None]) & (cols[None, :] < cstart[:, None] + WIN_C)
    dc_idx = jnp.clip(cols[None, :] - cols[:, None] + WIN_C - 1, 0, 2 * WIN_C - 2)
    bias_col = jnp.where(col_ok[None, None], rpb.astype(jnp.float32)[:, :, dc_idx], NEG_INF)
    scale = HEAD_DIM ** -0.5
    qg = (q * scale).reshape(B, rows, GRID_W, H, Dh)
    kg = k.reshape(B, rows, GRID_W, H, Dh)
    vg = v.reshape(B, rows, GRID_W, H, Dh)

    def row_block(r):
        rstart = jnp.clip(r - kr // 2, 0, rows - kr)
        q_r = lax.dynamic_index_in_dim(qg, r, axis=1, keepdims=False)
        k_r = lax.dynamic_slice_in_dim(kg, rstart, kr, axis=1)
        v_r = lax.dynamic_slice_in_dim(vg, rstart, kr, axis=1)
        dr_idx = rstart + jnp.arange(kr) - r + WIN_R - 1
        bias = jnp.take(bias_col, dr_idx, axis=1)
        s = jnp.einsum('bqhd,bkwhd->bhkqw', q_r, k_r,
                       preferred_element_type=jnp.float32) + bias
        pr = jax.nn.softmax(s, axis=(2, 4)).astype(v.dtype)
        return jnp.einsum('bhkqw,bkwhd->bqhd', pr, v_r)

    out = lax.map(row_block, jnp.arange(rows))
    return out.transpose(1, 0, 2, 3, 4).reshape(B, S, H * Dh)


def linear_scan(a, b):
    def combine(l, r):
        return (l[0] * r[0], r[0] * l[1] + r[1])
    return lax.associative_scan(combine, (a, b), axis=1)[1]


def rglru_bidirectional(xr, conv_w, conv_b, w_a, b_a, w_i, b_i, lam):
    B, S, _ = xr.shape
    left = CONV_W // 2
    xp = jnp.pad(xr, ((0, 0), (left, CONV_W - 1 - left), (0, 0)))
    xc = conv_b + xp[:, 0:S] * conv_w[0]
    for j in range(1, CONV_W):
        xc = xc + xp[:, j:j + S] * conv_w[j]
    xb = xc.reshape(B, S, REC_BLOCKS, REC_BLOCK_W)
    r_gate = jax.nn.sigmoid(
        jnp.einsum('bsnc,zncd->zbsnd', xb, w_a).reshape(2, B, S, D_REC) + b_a[:, None, None, :])
    i_gate = jax.nn.sigmoid(
        jnp.einsum('bsnc,zncd->zbsnd', xb, w_i).reshape(2, B, S, D_REC) + b_i[:, None, None, :])
    log_a = -C_RG * r_gate.astype(jnp.float32) * jax.nn.softplus(-lam.astype(jnp.float32))[:, None, None, :]
    a = jnp.exp(log_a)
    bterm = jnp.sqrt(-jnp.expm1(2.0 * log_a)) * i_gate.astype(jnp.float32) * xc.astype(jnp.float32)[None]
    h_fwd = linear_scan(a[0], bterm[0])
    h_bwd = jnp.flip(linear_scan(jnp.flip(a[1], axis=1), jnp.flip(bterm[1], axis=1)), axis=1)
    return (h_fwd + h_bwd).astype(xr.dtype)


def setup_inputs(seed: int = 0) -> dict:
    key = jax.random.key(seed)
    ks = jax.random.split(key, 32)
    f32 = jnp.float32

    def nrm(k, shape, scale):
        return jax.random.normal(k, shape, f32) * scale

    def gain(k, n):
        return 1.0 + 0.02 * jax.random.normal(k, (DEPTH, n), f32)

    u = jax.random.uniform(ks[10], (DEPTH, 2, D_REC), f32, minval=0.9, maxval=0.999)
    a0 = u ** (1.0 / C_RG)
    lam = jnp.log(a0) - jnp.log1p(-a0)
    return {
        "x": nrm(ks[0], (BATCH, SEQ, D_MODEL), 1.0),
        "p": nrm(ks[1], (DEPTH, BATCH, SEQ, D_PLE), 1.0),
        "g_mix_pre": gain(ks[2], D_MODEL),
        "w_in": nrm(ks[3], (DEPTH, D_MODEL, D_IN_PROJ), D_MODEL ** -0.5),
        "rpb": nrm(ks[4], (DEPTH, N_ATTN_HEADS, 2 * WIN_R - 1, 2 * WIN_C - 1), 0.02),
        "conv_w": nrm(ks[5], (DEPTH, CONV_W, D_REC), CONV_W ** -0.5),
        "conv_b": nrm(ks[6], (DEPTH, D_REC), 0.01),
        "w_rg_a": nrm(ks[7], (DEPTH, 2, REC_BLOCKS, REC_BLOCK_W, REC_BLOCK_W), REC_BLOCK_W ** -0.5),
        "b_rg_a": nrm(ks[8], (DEPTH, 2, D_REC), 0.01),
        "w_rg_i": nrm(ks[9], (DEPTH, 2, REC_BLOCKS, REC_BLOCK_W, REC_BLOCK_W), REC_BLOCK_W ** -0.5),
        "b_rg_i": nrm(ks[11], (DEPTH, 2, D_REC), 0.01),
        "lam": lam,
        "g_attn_out": gain(ks[12], D_ATTN),
        "g_rec_out": gain(ks[13], D_REC),
        "w_out": nrm(ks[14], (DEPTH, D_ATTN + D_REC, D_MODEL), (D_ATTN + D_REC) ** -0.5),
        "g_mix_post": gain(ks[15], D_MODEL),
        "g_ffn_pre": gain(ks[16], D_MODEL),
        "w_ffn_gate": nrm(ks[17], (DEPTH, D_MODEL, D_FF), D_MODEL ** -0.5),
        "w_ffn_up": nrm(ks[18], (DEPTH, D_MODEL, D_FF), D_MODEL ** -0.5),
        "w_ffn_down": nrm(ks[19], (DEPTH, D_FF, D_MODEL), D_FF ** -0.5),
        "g_ffn_post": gain(ks[20], D_MODEL),
        "g_ple_pre": gain(ks[21], D_MODEL),
        "w_ple_gate": nrm(ks[22], (DEPTH, D_MODEL, D_MODEL), D_MODEL ** -0.5),
        "w_ple_proj": nrm(ks[23], (DEPTH, D_PLE, D_MODEL), D_PLE ** -0.5),
        "g_ple_post": gain(ks[24], D_MODEL),
    }


def reference(x, p, g_mix_pre, w_in, rpb, conv_w, conv_b, w_rg_a, b_rg_a, w_rg_i, b_rg_i,
              lam, g_attn_out, g_rec_out, w_out, g_mix_post, g_ffn_pre, w_ffn_gate,
              w_ffn_up, w_ffn_down, g_ffn_post, g_ple_pre, w_ple_gate, w_ple_proj, g_ple_post):
    B, S, _ = x.shape
    h = x
    for i in range(DEPTH):
        hn = rms_norm(h, g_mix_pre[i])
        u = hn @ w_in[i]
        q, k, v, xr, yg = jnp.split(
            u, [D_ATTN, 2 * D_ATTN, 3 * D_ATTN, 3 * D_ATTN + D_REC], axis=-1)
        attn = neighborhood_attention(
            q.reshape(B, S, N_ATTN_HEADS, HEAD_DIM),
            k.reshape(B, S, N_ATTN_HEADS, HEAD_DIM),
            v.reshape(B, S, N_ATTN_HEADS, HEAD_DIM), rpb[i])
        rec = rglru_bidirectional(xr, conv_w[i], conv_b[i], w_rg_a[i], b_rg_a[i],
                                  w_rg_i[i], b_rg_i[i], lam[i]) * jax.nn.gelu(yg)
        mixed = jnp.concatenate(
            [rms_norm(attn, g_attn_out[i]), rms_norm(rec, g_rec_out[i])], axis=-1) @ w_out[i]
        h = h + rms_norm(mixed, g_mix_post[i])
        fn = rms_norm(h, g_ffn_pre[i])
        ff = (jax.nn.silu(fn @ w_ffn_gate[i]) * (fn @ w_ffn_up[i])) @ w_ffn_down[i]
        h = h + rms_norm(ff, g_ffn_post[i])
        gate = jax.nn.sigmoid(rms_norm(h, g_ple_pre[i]) @ w_ple_gate[i])
        ple = p[i] @ w_ple_proj[i]
        h = h + rms_norm(gate * ple, g_ple_post[i])
    return h
```

```python
import os
import numpy as np
import concourse.bass as bass
import concourse.mybir as mybir
from concourse.bass_utils import run_bass_kernel_spmd

F32 = mybir.dt.float32
BF16 = mybir.dt.bfloat16
AF = mybir.ActivationFunctionType
ALU = mybir.AluOpType

D = 2048
T = 512
NT_ALL = 16
NL = 8
DFF = 5632
NFF = DFF // 128
EPS = 1e-6
SCALE = 128 ** -0.5

SM = {}
_off = 0
for _n, _w in [("g_mix_pre", 16), ("g_attn", 8), ("g_rec", 8), ("g_mix_post", 16), ("g_ffn_pre", 16),
               ("g_ffn_post", 16), ("g_ple_pre", 16), ("g_ple_post", 16), ("conv_b", 8), ("cw5", 40),
               ("b_a", 16), ("b_i", 16), ("lam", 16)]:
    SM[_n] = _off
    _off += _w
NS = _off


class Op:
    __slots__ = ("eng", "fn", "deps", "sig", "sem", "val", "dma")

    def __init__(self, eng, fn, dma):
        self.eng = eng
        self.fn = fn
        self.dma = dma
        self.deps = ()
        self.sig = False
        self.sem = None
        self.val = 0


ENGS = ["pe", "act", "dve", "pool", "sp"]


class Rec:
    def __init__(self, nc):
        self.nc = nc
        self.ops = {e: [] for e in ENGS}
        self.lastw = {}
        self.readers = {}
        self.dma_cnt = {}
        self.dma_since = []
        self.keep = set()
        self.keep_dma = set()
        self.engsem = {}
        self.engcnt = {e: 0 for e in ENGS}
        self.dmasem = {}

    def op(self, eng, fn, reads=(), writes=(), dma=None):
        o = Op(eng, fn, dma)
        deps = set()
        for r in reads:
            w = self.lastw.get(r)
            if w is not None:
                deps.add(w)
        for k in writes:
            w = self.lastw.get(k)
            if w is not None:
                deps.add(w)
            for rd in self.readers.get(k, ()):
                deps.add(rd)
        o.deps = deps
        for r in reads:
            lst = self.readers.setdefault(r, [])
            if dma is None:
                for n_, x_ in enumerate(lst):
                    if x_.eng == eng and x_.dma is None:
                        lst[n_] = o
                        break
                else:
                    lst.append(o)
            else:
                lst.append(o)
        for k in writes:
            self.lastw[k] = o
            self.readers[k] = []
        if dma is not None:
            c = self.dma_cnt.get(dma, 0) + 1
            self.dma_cnt[dma] = c
            o.val = 16 * c
            self.dma_since.append(o)
        self.ops[eng].append(o)
        return o

    def barrier(self):
        pend = set(o for o in self.dma_since if o.dma not in self.keep_dma)
        for e in ENGS:
            for o in reversed(self.ops[e]):
                if o.fn is not None and o.dma is None:
                    pend.add(o)
                    break
        self.dma_since = []
        for e in ENGS:
            b = Op(e, None, None)
            b.deps = set(pend)
            self.ops[e].append(b)
        keepw = {k: v for k, v in self.lastw.items() if k in self.keep}
        self.lastw = keepw
        self.readers = {}

    def emit(self, es):
        nc = self.nc
        for e in ENGS:
            for o in self.ops[e]:
                for d in o.deps:
                    if d.dma is None and not (d.eng == "pe" and o.eng == "pe"):
                        d.sig = True
        engsem = self.engsem
        for e in ENGS:
            if e not in engsem:
                engsem[e] = es.enter_context(nc.semaphore("eng_" + e))
            c = self.engcnt[e]
            for o in self.ops[e]:
                if o.dma is None and o.sig and o.sem is None:
                    c += 1
                    o.val = c
                    o.sem = engsem[e]
            self.engcnt[e] = c
        dmasem = self.dmasem
        for k in self.dma_cnt:
            if k not in dmasem:
                dmasem[k] = es.enter_context(nc.semaphore("dma_%d" % len(dmasem)))
        for e in ENGS:
            for o in self.ops[e]:
                if o.dma is not None:
                    o.sem = dmasem[o.dma]
        ops = self.ops
        self.ops = {e: [] for e in ENGS}
        with nc.Block() as block:
            self._emit_block(block, ops)

    def _emit_block(self, block, ops):

        def run(e, eng):
            waited = {}
            for o in ops[e]:
                for d in o.deps:
                    if d.eng == e and d.dma is None and e == "pe":
                        continue
                    if d.eng == e and d.dma is None and d.fn is None:
                        continue
                    k = id(d.sem)
                    if waited.get(k, 0) >= d.val:
                        continue
                    eng.wait_ge(d.sem, d.val)
                    waited[k] = d.val
                if o.fn is None:
                    continue
                ins = o.fn(eng)
                if o.dma is not None:
                    ins.then_inc(o.sem, 16)
                elif o.sig:
                    ins.then_inc(o.sem, 1)

        @block.tensor
        def _(eng):
            run("pe", eng)

        @block.scalar
        def _(eng):
            run("act", eng)

        @block.vector
        def _(eng):
            run("dve", eng)

        @block.gpsimd
        def _(eng):
            run("pool", eng)

        @block.sync
        def _(eng):
            run("sp", eng)


def interleave(a, b):
    out = []
    na, nb = len(a), len(b)
    ia = ib = 0
    while ia < na or ib < nb:
        if ib >= nb or (ia < na and ia * nb <= ib * na):
            out.append(a[ia])
            ia += 1
        else:
            out.append(b[ib])
            ib += 1
    return out


def build_nc():
    from contextlib import ExitStack
    nc = bass.Bass("TRN2", target_bir_lowering=False)
    STOP = int(os.environ.get("K_STOP", "9"))

    def din(name, shape, dt=F32):
        return nc.dram_tensor(name, shape, dt, kind="ExternalInput").ap()

    def dscr(name, shape, dt):
        return nc.dram_tensor(name, shape, dt, kind="Internal").ap()

    xT = din("xT", [D, 8192])
    pT = din("pT", [256, 4096])
    sm_d = din("sm", [128, NS])
    wa_d = din("wa", [128, 2 * 8 * 128])
    wi_d = din("wi", [128, 2 * 8 * 128])
    bg_d = din("biasG", [8, 128, 6 * 320])
    bm_d = din("biasM", [128, 6 * 320])
    w_in = din("w_in", [D, 5120])
    w_out = din("w_out", [D, D])
    w_g = din("w_g", [D, DFF])
    w_u = din("w_u", [D, DFF])
    w_d = din("w_d", [DFF, D])
    w_pg = din("w_pg", [D, D])
    w_pp = din("w_pp", [256, D])
    outT = nc.dram_tensor("outT", [D, 4096], F32, kind="ExternalOutput").ap()

    wb_in = dscr("wb_in", [D, 5120], BF16)
    wb_out = dscr("wb_out", [D, D], BF16)
    wb_g = dscr("wb_g", [D, DFF], BF16)
    wb_u = dscr("wb_u", [D, DFF], BF16)
    wb_d = dscr("wb_d", [16, 128, NFF * 128], BF16)
    wb_pg = dscr("wb_pg", [D, D], BF16)
    wb_pp = dscr("wb_pp", [256, D], BF16)
    qT_s = dscr("qT_s", [1024, 4096], BF16)
    kT_s = dscr("kT_s", [1024, 4608], BF16)
    v_s = dscr("v_s", [4608, 1024], BF16)
    xr_s = dscr("xr_s", [1024, 8192 + 4], F32)
    gy_s = dscr("gy_s", [1024, 4096], F32)
    hs_s = dscr("hs_s", [1024, 4096], F32)
    af_s = dscr("af_s", [1024, 4096], F32)
    bf_s = dscr("bf_s", [1024, 4096], F32)
    rec_s = dscr("rec_s", [1024, 4096], F32)
    at_s = dscr("at_s", [1024, 4096], F32)

    R = Rec(nc)
    es = ExitStack()

    def sb(name, shape, dt):
        return es.enter_context(nc.sbuf_tensor("s_" + name, shape, dt))

    psb = [es.enter_context(nc.psum_tensor("ps%d" % i, [128, 512], F32)) for i in range(8)]
    sm = sb("sm", [128, NS], F32)
    cst = sb("cst", [128, 4], F32)
    ones_bf = sb("ones_bf", [128, 128], BF16)
    ident_bf = sb("ident_bf", [128, 128], BF16)
    ident_f = sb("ident_f", [128, 128], F32)
    sl = sb("sl", [128, 16], F32)
    wsl = [sb("wsl%d" % i, [128, 8192], BF16) for i in range(3)]
    sq = [sb("sq%d" % i, [128, T], BF16) for i in range(4)]
    rstd = [sb("rstd%d" % i, [128, T], F32) for i in range(2)]
    tmpf = [sb("tmpf%d" % i, [128, T], F32) for i in range(2)]

    cnt = {"ps": 0, "w": 0, "sq": 0, "rstd": 0, "tmpf": 0}

    def nxt(key, n):
        v = cnt.get(key, 0)
        cnt[key] = v + 1
        return v % n

    def ps_next():
        return nxt("ps", 6)

    R.op("sp", lambda e: e.dma_start(out=sm[:], in_=sm_d[:, :]), writes=["sm"], dma="sm")
    R.op("dve", lambda e: e.memset(cst[:, 0:1], 1.0), writes=["cst"])
    R.op("dve", lambda e: e.memset(cst[:, 1:2], EPS), writes=["cst"])
    R.op("dve", lambda e: e.memset(cst[:, 2:3], 0.0), writes=["cst"])
    R.op("dve", lambda e: e.memset(ones_bf[:], 1.0), writes=["ones"])
    ident_d = din("ident", [128, 128])
    R.op("sp", lambda e: e.dma_start(out=ident_f[:], in_=ident_d[:, :]), writes=["identf"], dma="identd")
    R.op("dve", lambda e: e.tensor_copy(out=ident_bf[:], in_=ident_f[:]), reads=["identf"], writes=["ident"])
    lo = SM["lam"]
    R.op("act", lambda e: e.activation(out=sl[:], in_=sm[:, lo:lo + 16], func=AF.Exp, scale=-1.0),
         reads=["sm"], writes=["sl"])
    R.op("act", lambda e: e.activation(out=sl[:], in_=sl[:], func=AF.Ln, bias=cst[:, 0:1], scale=1.0),
         reads=["sl", "cst"], writes=["sl"])
    R.op("dve", lambda e: e.tensor_scalar(out=sl[:], in0=sl[:], scalar1=-8.0, scalar2=None, op0=ALU.mult),
         reads=["sl"], writes=["sl"])

    esS = ExitStack()
    NCS = 3
    cfb = [esS.enter_context(nc.sbuf_tensor("c_cf%d" % i, [128, 2048], F32)) for i in range(NCS)]
    cbb = [esS.enter_context(nc.sbuf_tensor("c_cb%d" % i, [128, 2048], BF16)) for i in range(NCS)]
    castn = [0]

    def CW(key):
        return [(key, s_) for s_ in range(NCS)]

    def cast_block(src_ap, w, dst_fn, key):
        s_ = castn[0] % NCS
        castn[0] += 1
        R.op("sp", lambda e: e.dma_start(out=cfb[s_][:, 0:w], in_=src_ap), writes=[("cf", s_)], dma=("cf", s_))
        eng = ["pool", "act", "dve"][s_]
        if eng == "act":
            R.op("act", lambda e: e.activation(out=cbb[s_][:, 0:w], in_=cfb[s_][:, 0:w], func=AF.Copy),
                 reads=[("cf", s_)], writes=[("cb", s_)])
        else:
            R.op(eng, lambda e: e.tensor_copy(out=cbb[s_][:, 0:w], in_=cfb[s_][:, 0:w]),
                 reads=[("cf", s_)], writes=[("cb", s_)])
        R.op("sp", lambda e: e.dma_start(out=dst_fn(cbb[s_])[0], in_=dst_fn(cbb[s_])[1]),
             reads=[("cb", s_)], writes=[(key, s_)], dma=("cbst", s_))

    def cast2d(src, dst, K, N, key):
        for kc in range(K // 128):
            c0 = 0
            while c0 < N:
                w = min(2048, N - c0)
                cast_block(src[kc * 128:(kc + 1) * 128, c0:c0 + w], w,
                           lambda t, kc=kc, c0=c0, w=w: (dst[kc * 128:(kc + 1) * 128, c0:c0 + w], t[:, 0:w]),
                           key)
                c0 += w

    cast2d(w_in, wb_in, D, 5120, "c_in")
    cast2d(w_out, wb_out, D, D, "c_out")
    cast2d(w_g, wb_g, D, DFF, "c_g")
    cast2d(w_u, wb_u, D, DFF, "c_u")
    wbd_v = wb_d.rearrange("m p (f n) -> p m f n", n=128)
    for f_ in range(NFF):
        cast_block(w_d[f_ * 128:(f_ + 1) * 128, :], 2048,
                   lambda t, f_=f_: (wbd_v[:, :, f_, :], t[:, :].rearrange("p (m n) -> p m n", n=128)), "c_d")
    cast2d(w_pg, wb_pg, D, D, "c_pg")
    cast2d(w_pp, wb_pp, 256, D, "c_pp")

    def wload(src3, kcn, ncols, deps):
        s = nxt("w", 3)
        view = wsl[s][:, 0:kcn * ncols].rearrange("p (k n) -> p k n", n=ncols)
        R.op("sp", lambda e: e.dma_start(out=view, in_=src3), reads=deps, writes=[("w", s)], dma=("w", s))
        return s, view

    def rms_accum(src_ap_fn, nchunks, src_keys, ssbank):
        for c in range(nchunks):
            q = nxt("sq", 4)
            R.op("act", lambda e, c=c, q=q: e.activation(out=sq[q][:], in_=src_ap_fn(c), func=AF.Square),
                 reads=[src_keys(c)], writes=[("sq", q)])
            R.op("pe", lambda e, c=c, q=q: e.matmul(psb[ssbank][:], lhsT=ones_bf[:], rhs=sq[q][:],
                                                     start=(c == 0), stop=(c == nchunks - 1)),
                 reads=[("sq", q)], writes=[("ps", ssbank)])

    def sq_accum_one(src_ap, src_key, ssbank, first, last):
        q = nxt("sq", 4)
        R.op("act", lambda e: e.activation(out=sq[q][:], in_=src_ap, func=AF.Square),
             reads=[src_key], writes=[("sq", q)])
        R.op("pe", lambda e: e.matmul(psb[ssbank][:], lhsT=ones_bf[:], rhs=sq[q][:], start=first, stop=last),
             reads=[("sq", q)], writes=[("ps", ssbank)])

    def rstd_from(ssbank, dim):
        r = nxt("rstd", 2)
        R.op("act", lambda e: e.activation(out=rstd[r][:], in_=psb[ssbank][:], func=AF.Sqrt,
                                           bias=cst[:, 1:2], scale=1.0 / dim),
             reads=[("ps", ssbank)], writes=[("rstd", r)])
        R.op("dve", lambda e: e.reciprocal(out=rstd[r][:], in_=rstd[r][:]),
             reads=[("rstd", r)], writes=[("rstd", r)])
        return r

    R.keep = set(k_ for n_ in ["c_in", "c_out", "c_g", "c_u", "c_d", "c_pg", "c_pp"] for k_ in CW(n_))
    R.barrier()
    R.emit(es)
    esS.close()
    if STOP == 1:
        es.close()
        return nc

    esA = ExitStack()

    def sbA(name, shape, dt):
        return esA.enter_context(nc.sbuf_tensor("a_" + name, shape, dt))

    xt = sbA("xt", [128, 16, T], F32)
    hn = sbA("hn", [128, 16, T], BF16)
    stb = [sbA("stb%d" % i, [128, T], BF16) for i in range(4)]
    stf = [sbA("stf%d" % i, [128, T], F32) for i in range(4)]
    xrw = sbA("xrw", [128, 8, T + 4], F32)
    xc = sbA("xc", [128, 8, T], F32)
    xcb = sbA("xcb", [128, 8, T], BF16)
    wa_sb = sbA("wa_sb", [128, 2 * 8 * 128], BF16)
    wi_sb = sbA("wi_sb", [128, 2 * 8 * 128], BF16)
    zpad = sbA("zpad", [128, 8, 2], F32)
    state_s = sbA("state_s", [128, 8], F32)
    tr = [sbA("tr%d" % i, [128, T], F32) for i in range(2)]
    ti = [sbA("ti%d" % i, [128, T], F32) for i in range(2)]
    ta = [sbA("ta%d" % i, [128, T], F32) for i in range(2)]
    tq = [sbA("tq%d" % i, [128, T], F32) for i in range(2)]
    tm = [sbA("tm%d" % i, [128, T], F32) for i in range(2)]
    tb_ = [sbA("tb%d" % i, [128, T], F32) for i in range(2)]
    th = [sbA("th%d" % i, [128, T], F32) for i in range(2)]

    R.op("sp", lambda e: e.dma_start(out=xt[:, 0:4, :], in_=wa_d.rearrange("p (a b) -> p a b", b=T)),
         writes=["xta"], dma="wab0")
    R.op("sp", lambda e: e.dma_start(out=xt[:, 4:8, :], in_=wi_d.rearrange("p (a b) -> p a b", b=T)),
         writes=["xtb"], dma="wab1")
    for a_ in range(4):
        R.op("act", lambda e, a_=a_: e.activation(out=wa_sb[:, a_ * T:(a_ + 1) * T], in_=xt[:, a_, :],
                                                  func=AF.Copy), reads=["xta"], writes=["wa"])
        R.op("act", lambda e, a_=a_: e.activation(out=wi_sb[:, a_ * T:(a_ + 1) * T], in_=xt[:, 4 + a_, :],
                                                  func=AF.Copy), reads=["xtb"], writes=["wi"])
    R.op("dve", lambda e: e.memset(zpad[:], 0.0), writes=["zpad"])
    R.op("dve", lambda e: e.memset(state_s[:], 0.0), writes=[("st", c) for c in range(8)])
    xr_v = xr_s.rearrange("(c p) t -> p c t", p=128)
    R.op("sp", lambda e: e.dma_start(out=xr_v[:, :, 0:2], in_=zpad[:]), reads=["zpad"], writes=[("xr", -1)],
         dma="xrpad0")
    R.op("sp", lambda e: e.dma_start(out=xr_v[:, :, 8194:8196], in_=zpad[:]), reads=["zpad"],
         writes=[("xr", 16)], dma="xrpad1")

    xT_v = xT.rearrange("(c p) t -> p c t", p=128)
    win_v = wb_in.rearrange("(k p) n -> p k n", p=128)
    qT_v = qT_s
    g0 = SM["g_mix_pre"]

    def s1_head(i):
        def u():
            R.op("sp", lambda e: e.dma_start(out=xt[:], in_=xT_v[:, :, i * T:(i + 1) * T]),
                 writes=["xt", "xta", "xtb"], dma="xt")
            rms_accum(lambda c: xt[:, c, :], 16, lambda c: "xt", 6)
            r = rstd_from(6, float(D))
            for c in range(16):
                R.op("dve", lambda e, c=c: e.scalar_tensor_tensor(
                    out=hn[:, c, :], in0=xt[:, c, :], scalar=sm[:, g0 + c:g0 + c + 1], in1=rstd[r][:],
                    op0=ALU.mult, op1=ALU.mult), reads=["xt", ("rstd", r)], writes=["hn"])
        return u

    def s1_group(i, g):
        def u():
            s, wv = wload(win_v[:, :, g * 512:(g + 1) * 512], 16, 512, CW("c_in"))
            kind = ["q", "q", "k", "k", "v", "v", "xr", "xr", "yg", "yg"][g]
            gi = g % 2
            li = i - 8
            if kind == "v":
                for tb in range(4):
                    p = ps_next()
                    for kc in range(16):
                        R.op("pe", lambda e, kc=kc, tb=tb, p=p: e.matmul(
                            psb[p][:], lhsT=hn[:, kc, tb * 128:(tb + 1) * 128], rhs=wv[:, kc, :],
                            start=(kc == 0), stop=(kc == 15)), reads=["hn", ("w", s)], writes=[("ps", p)])
                    b = nxt("stb", 4)
                    R.op("dve", lambda e, p=p, b=b: e.tensor_copy(out=stb[b][:], in_=psb[p][:]),
                         reads=[("ps", p)], writes=[("stb", b)])
                    row0 = (i - 7) * T + tb * 128
                    R.op("sp", lambda e, b=b, row0=row0: e.dma_start(
                        out=v_s[row0:row0 + 128, gi * 512:(gi + 1) * 512], in_=stb[b][:]),
                        reads=[("stb", b)], writes=[("v_s", i)], dma=("stb", b))
                return
            for m in range(4):
                p = ps_next()
                for kc in range(16):
                    R.op("pe", lambda e, kc=kc, m=m, p=p: e.matmul(
                        psb[p][:], lhsT=wv[:, kc, m * 128:(m + 1) * 128], rhs=hn[:, kc, :],
                        start=(kc == 0), stop=(kc == 15)), reads=["hn", ("w", s)], writes=[("ps", p)])
                ch = gi * 4 + m
                if kind == "q":
                    b = nxt("stb", 4)
                    R.op("act", lambda e, p=p, b=b: e.activation(out=stb[b][:], in_=psb[p][:], func=AF.Copy,
                                                                 scale=SCALE),
                         reads=[("ps", p)], writes=[("stb", b)])
                    R.op("sp", lambda e, b=b, ch=ch: e.dma_start(
                        out=qT_s[ch * 128:(ch + 1) * 128, li * T:(li + 1) * T], in_=stb[b][:]),
                        reads=[("stb", b)], writes=[("q_s", li)], dma=("stb", b))
                elif kind == "k":
                    b = nxt("stb", 4)
                    R.op("dve", lambda e, p=p, b=b: e.tensor_copy(out=stb[b][:], in_=psb[p][:]),
                         reads=[("ps", p)], writes=[("stb", b)])
                    c0 = (i - 7) * T
                    R.op("sp", lambda e, b=b, ch=ch, c0=c0: e.dma_start(
                        out=kT_s[ch * 128:(ch + 1) * 128, c0:c0 + T], in_=stb[b][:]),
                        reads=[("stb", b)], writes=[("k_s", i)], dma=("stb", b))
                elif kind == "xr":
                    b = nxt("stf", 4)
                    R.op("dve", lambda e, p=p, b=b: e.tensor_copy(out=stf[b][:], in_=psb[p][:]),
                         reads=[("ps", p)], writes=[("stf", b)])
                    c0 = 2 + i * T
                    R.op("sp", lambda e, b=b, ch=ch, c0=c0: e.dma_start(
                        out=xr_s[ch * 128:(ch + 1) * 128, c0:c0 + T], in_=stf[b][:]),
                        reads=[("stf", b)], writes=[("xr", i)], dma=("stf", b))
                else:
                    b = nxt("stf", 4)
                    R.op("act", lambda e, p=p, b=b: e.activation(out=stf[b][:], in_=psb[p][:],
                                                                 func=AF.Gelu_apprx_tanh),
                         reads=[("ps", p)], writes=[("stf", b)])
                    R.op("sp", lambda e, b=b, ch=ch: e.dma_start(
                        out=gy_s[ch * 128:(ch + 1) * 128, li * T:(li + 1) * T], in_=stf[b][:]),
                        reads=[("stf", b)], writes=[("gy_s", li)], dma=("stf", b))
        return u

    cb0 = SM["conv_b"]
    cw0 = SM["cw5"]
    ba0 = SM["b_a"]
    bi0 = SM["b_i"]

    def s2_units(i):
        units = []
        local = i >= 8
        li = i - 8

        def conv():
            R.op("sp", lambda e: e.dma_start(out=xrw[:], in_=xr_v[:, :, i * T:i * T + T + 4]),
                 reads=[("xr", i - 1), ("xr", i), ("xr", i + 1)], writes=["xrw"], dma="xrw")
            for c in range(8):
                R.op("dve", lambda e, c=c: e.tensor_scalar(
                    out=xc[:, c, :], in0=xrw[:, c, 0:T], scalar1=sm[:, cw0 + c * 5:cw0 + c * 5 + 1],
                    scalar2=sm[:, cb0 + c:cb0 + c + 1], op0=ALU.mult, op1=ALU.add),
                    reads=["xrw"], writes=[("xc", c)])
            for o in range(1, 5):
                for c in range(8):
                    R.op("dve", lambda e, c=c, o=o: e.scalar_tensor_tensor(
                        out=xc[:, c, :], in0=xrw[:, c, o:o + T],
                        scalar=sm[:, cw0 + c * 5 + o:cw0 + c * 5 + o + 1], in1=xc[:, c, :],
                        op0=ALU.mult, op1=ALU.add), reads=["xrw", ("xc", c)], writes=[("xc", c)])
            for c in range(8):
                R.op("act", lambda e, c=c: e.activation(out=xcb[:, c, :], in_=xc[:, c, :], func=AF.Copy),
                     reads=[("xc", c)], writes=[("xcb", c)])
        units.append(conv)

        def gate(z, c):
            def u():
                zc = z * 8 + c
                pr = ps_next()
                R.op("pe", lambda e: e.matmul(psb[pr][:], lhsT=wa_sb[:, zc * 128:(zc + 1) * 128],
                                              rhs=xcb[:, c, :], start=True, stop=True),
                     reads=[("xcb", c), "wa"], writes=[("ps", pr)])
                pi = ps_next()
                R.op("pe", lambda e: e.matmul(psb[pi][:], lhsT=wi_sb[:, zc * 128:(zc + 1) * 128],
                                              rhs=xcb[:, c, :], start=True, stop=True),
                     reads=[("xcb", c), "wi"], writes=[("ps", pi)])
                k = nxt("gt", 2)
                R.op("act", lambda e: e.activation(out=tr[k][:], in_=psb[pr][:], func=AF.Sigmoid,
                                                   bias=sm[:, ba0 + zc:ba0 + zc + 1], scale=1.0),
                     reads=[("ps", pr)], writes=[("tr", k)])
                R.op("act", lambda e: e.activation(out=ti[k][:], in_=psb[pi][:], func=AF.Sigmoid,
                                                   bias=sm[:, bi0 + zc:bi0 + zc + 1], scale=1.0),
                     reads=[("ps", pi)], writes=[("ti", k)])
                R.op("act", lambda e: e.activation(out=ta[k][:], in_=tr[k][:], func=AF.Exp,
                                                   scale=sl[:, zc:zc + 1]),
                     reads=[("tr", k)], writes=[("ta", k)])
                R.op("dve", lambda e: e.tensor_tensor(out=tq[k][:], in0=ta[k][:], in1=ta[k][:], op=ALU.mult),
                     reads=[("ta", k)], writes=[("tq", k)])
                R.op("act", lambda e: e.activation(out=tm[k][:], in_=tq[k][:], func=AF.Sqrt,
                                                   bias=cst[:, 0:1], scale=-1.0),
                     reads=[("tq", k)], writes=[("tm", k)])
                R.op("dve", lambda e: e.tensor_tensor(out=tb_[k][:], in0=tm[k][:], in1=ti[k][:], op=ALU.mult),
                     reads=[("tm", k), ("ti", k)], writes=[("tb", k)])
                R.op("dve", lambda e: e.tensor_tensor(out=tb_[k][:], in0=tb_[k][:], in1=xc[:, c, :],
                                                      op=ALU.mult),
                     reads=[("tb", k), ("xc", c)], writes=[("tb", k)])
                if z == 0:
                    R.op("dve", lambda e: e.tensor_tensor_scan(
                        out=th[k][:], data0=ta[k][:], data1=tb_[k][:], initial=state_s[:, c:c + 1],
                        op0=ALU.mult, op1=ALU.add),
                        reads=[("ta", k), ("tb", k), ("st", c)], writes=[("th", k)])
                    R.op("dve", lambda e: e.tensor_copy(out=state_s[:, c:c + 1], in_=th[k][:, T - 1:T]),
                         reads=[("th", k)], writes=[("st", c)])
                    if local:
                        R.op("sp", lambda e: e.dma_start(
                            out=hs_s[c * 128:(c + 1) * 128, li * T:(li + 1) * T], in_=th[k][:]),
                            reads=[("th", k)], writes=[("hs_s", li)], dma=("th", k))
                else:
                    R.op("sp", lambda e: e.dma_start(
                        out=af_s[c * 128:(c + 1) * 128, li * T:(li + 1) * T], in_=ta[k][:]),
                        reads=[("ta", k)], writes=[("af_s", li)], dma=("ta", k))
                    R.op("sp", lambda e: e.dma_start(
                        out=bf_s[c * 128:(c + 1) * 128, li * T:(li + 1) * T], in_=tb_[k][:]),
                        reads=[("tb", k)], writes=[("bf_s", li)], dma=("tb", k))
            return u
        for c in range(8):
            units.append(gate(0, c))
            if local:
                units.append(gate(1, c))
        return units

    for i in range(NT_ALL):
        s1_head(i)()
        groups_first = [6, 7]
        rest = []
        if i == 7:
            rest = [2, 3, 4, 5]
        elif i >= 8:
            rest = [0, 1, 2, 3, 4, 5, 8, 9]
        for g in groups_first:
            s1_group(i, g)()
        ua = [s1_group(i, g) for g in rest]
        ub = s2_units(i - 1) if i >= 1 else []
        for u in interleave(ua, ub):
            u()
    for u in s2_units(15):
        u()

    R.barrier()
    R.emit(es)
    esA.close()
    if STOP == 2:
        es.close()
        return nc

    esB = ExitStack()

    def sbB(name, shape, dt):
        return esB.enter_context(nc.sbuf_tensor("b_" + name, shape, dt))

    afb = sbB("afb", [128, 8, T], F32)
    bfb = sbB("bfb", [128, 8, T], F32)
    hsc = [sbB("hsc%d" % i, [128, T], F32) for i in range(2)]
    gyc = [sbB("gyc%d" % i, [128, T], F32) for i in range(2)]
    state_f = sbB("state_f", [128, 8], F32)
    hf = [sbB("hf%d" % i, [128, T], F32) for i in range(2)]
    qh = [sbB("qh%d" % i, [128, 4096], BF16) for i in range(2)]
    kh = [sbB("kh%d" % i, [128, 4608], BF16) for i in range(2)]
    vh = [sbB("vh%d" % i, [128, 36, 128], BF16) for i in range(2)]
    bgf = sbB("bgf", [128, 6 * 320], F32)
    bmf = sbB("bmf", [128, 6 * 320], F32)
    bbf = [sbB("bbf%d" % i, [128, 6 * 320], BF16) for i in range(2)]
    pTt = [sbB("pTt%d" % i, [128, 320], BF16) for i in range(3)]
    rdn = [sbB("rdn%d" % i, [128, T], F32) for i in range(2)]
    aos = [sbB("aos%d" % i, [128, T], F32) for i in range(2)]

    R.op("dve", lambda e: e.memset(state_f[:], 0.0), writes=[("sf", c) for c in range(8)])
    R.op("sp", lambda e: e.dma_start(out=bmf[:], in_=bm_d[:, :]), writes=["bmf"], dma="bmf")

    def s3_tile(li):
        cs = slice(li * T, (li + 1) * T)

        def v(ap):
            return ap.rearrange("(c p) t -> p c t", p=128)[:, :, cs]
        R.op("sp", lambda e: e.dma_start(out=afb[:], in_=v(af_s)), writes=["afb"], dma="afb")
        R.op("sp", lambda e: e.dma_start(out=bfb[:], in_=v(bf_s)), writes=["bfb"], dma="bfb")
        for c in range(8):
            k = nxt("hf", 2)
            R.op("sp", lambda e, c=c, k=k: e.dma_start(out=hsc[k][:], in_=hs_s[c * 128:(c + 1) * 128, cs]),
                 writes=[("hsc", k)], dma=("hsc", k))
            R.op("sp", lambda e, c=c, k=k: e.dma_start(out=gyc[k][:], in_=gy_s[c * 128:(c + 1) * 128, cs]),
                 writes=[("gyc", k)], dma=("gyc", k))
            R.op("dve", lambda e, c=c, k=k: e.tensor_tensor_scan(
                out=hf[k][:, ::-1], data0=afb[:, c, ::-1], data1=bfb[:, c, ::-1],
                initial=state_f[:, c:c + 1], op0=ALU.mult, op1=ALU.add),
                reads=["afb", "bfb", ("sf", c)], writes=[("hf", k)])
            R.op("dve", lambda e, c=c, k=k: e.tensor_copy(out=state_f[:, c:c + 1], in_=hf[k][:, 0:1]),
                 reads=[("hf", k)], writes=[("sf", c)])
            R.op("pool", lambda e, c=c, k=k: e.tensor_tensor(out=hsc[k][:], in0=hf[k][:], in1=hsc[k][:],
                                                             op=ALU.add),
                 reads=[("hf", k), ("hsc", k)], writes=[("hsc", k)])
            R.op("pool", lambda e, c=c, k=k: e.tensor_tensor(out=hsc[k][:], in0=hsc[k][:], in1=gyc[k][:],
                                                             op=ALU.mult),
                 reads=[("hsc", k), ("gyc", k)], writes=[("hsc", k)])
            R.op("sp", lambda e, c=c, k=k: e.dma_start(out=rec_s[c * 128:(c + 1) * 128, cs], in_=hsc[k][:]),
                 reads=[("hsc", k)], writes=[("rec_s", li)], dma=("hsc", k))

    v_v = v_s.rearrange("(t p) f -> p t f", p=128)

    def attn_head(h):
        hp = h % 2
        R.op("sp", lambda e: e.dma_start(out=qh[hp][:], in_=qT_s[h * 128:(h + 1) * 128, :]),
             writes=[("qh", hp)], dma=("qh", hp))
        R.op("sp", lambda e: e.dma_start(out=kh[hp][:], in_=kT_s[h * 128:(h + 1) * 128, :]),
             writes=[("kh", hp)], dma=("kh", hp))
        R.op("sp", lambda e: e.dma_start(out=vh[hp][:, 0:18, :], in_=v_v[:, 0:18, h * 128:(h + 1) * 128]),
             writes=[("vh", hp)], dma=("vh", hp))
        R.op("sp", lambda e: e.dma_start(out=vh[hp][:, 18:36, :], in_=v_v[:, 18:36, h * 128:(h + 1) * 128]),
             writes=[("vh", hp)], dma=("vh", hp))
        R.op("sp", lambda e: e.dma_start(out=bgf[:], in_=bg_d[h]), writes=["bgf"], dma="bgf")
        R.op("dve", lambda e: e.tensor_tensor(out=bbf[hp][:], in0=bgf[:], in1=bmf[:], op=ALU.add),
             reads=["bgf", "bmf"], writes=[("bbf", hp)])
        for bq in range(8):
            po = 4 + bq % 2
            pd = 6 + bq % 2
            for jj in range(8):
                j = bq * 8 + jj
                if j < 60:
                    cls = j % 2
                    tbr = j - 4 - (j % 2)
                    ntl = 5
                else:
                    cls = 2 + (j - 60)
                    tbr = 56
                    ntl = 4
                srow = tbr + 8
                p = nxt("pss", 4)
                R.op("pe", lambda e, p=p, cls=cls, ntl=ntl: e.matmul(
                    psb[p][:, 0:ntl * 64], lhsT=ident_bf[:], rhs=bbf[hp][:, cls * 320:cls * 320 + ntl * 64],
                    start=True, stop=False, skip_group_check=True),
                    reads=[("bbf", hp)], writes=[("ps", p)])
                for t in range(ntl):
                    R.op("pe", lambda e, p=p, t=t, srow=srow, j=j, ntl=ntl: e.matmul(
                        psb[p][:, t * 64:(t + 1) * 64],
                        lhsT=kh[hp][:, (srow + 2 * t) * 64:(srow + 2 * t) * 64 + 128],
                        rhs=qh[hp][:, j * 64:(j + 1) * 64], start=False, stop=(t == ntl - 1),
                        skip_group_check=True),
                        reads=[("kh", hp), ("qh", hp)], writes=[("ps", p)])
                pt = nxt("pTt", 3)
                R.op("act", lambda e, p=p, pt=pt, ntl=ntl: e.activation(
                    out=pTt[pt][:, 0:ntl * 64], in_=psb[p][:, 0:ntl * 64], func=AF.Exp),
                    reads=[("ps", p)], writes=[("pTt", pt)])
                for t in range(ntl):
                    R.op("pe", lambda e, t=t, pt=pt, jj=jj, srow=srow, ntl=ntl, po=po: e.matmul(
                        psb[po][:, jj * 64:(jj + 1) * 64], lhsT=vh[hp][:, srow // 2 + t, :],
                        rhs=pTt[pt][:, t * 64:(t + 1) * 64], start=(t == 0), stop=(t == ntl - 1),
                        skip_group_check=True),
                        reads=[("vh", hp), ("pTt", pt)], writes=[("ps", po)])
                for t in range(ntl):
                    R.op("pe", lambda e, t=t, pt=pt, jj=jj, ntl=ntl, pd=pd: e.matmul(
                        psb[pd][:, jj * 64:(jj + 1) * 64], lhsT=ones_bf[:],
                        rhs=pTt[pt][:, t * 64:(t + 1) * 64], start=(t == 0), stop=(t == ntl - 1),
                        skip_group_check=True),
                        reads=[("pTt", pt)], writes=[("ps", pd)])
            k = nxt("rdn", 2)
            R.op("dve", lambda e, k=k, pd=pd: e.reciprocal(out=rdn[k][:], in_=psb[pd][:]),
                 reads=[("ps", pd)], writes=[("rdn", k)])
            R.op("dve", lambda e, k=k, po=po: e.tensor_tensor(out=aos[k][:], in0=psb[po][:], in1=rdn[k][:],
                                                              op=ALU.mult),
                 reads=[("ps", po), ("rdn", k)], writes=[("aos", k)])
            R.op("sp", lambda e, k=k, bq=bq: e.dma_start(
                out=at_s[h * 128:(h + 1) * 128, bq * T:(bq + 1) * T], in_=aos[k][:]),
                reads=[("aos", k)], writes=[("at_s", bq)], dma=("aos", k))

    for h in range(8):
        attn_head(h)
        s3_tile(7 - h)

    R.barrier()
    R.emit(es)
    esB.close()
    if STOP == 3:
        es.close()
        return nc

    H = sb("H", [128, 16, T], F32)
    B = sb("B", [128, 16, T], F32)
    Fb = sb("Fb", [128, 16, T], BF16)
    G = sb("G", [128, NFF, T], BF16)
    sg = [sb("sg%d" % i, [128, T], F32) for i in range(2)]
    pf = sb("pf", [128, 2, T], F32)
    pb = sb("pb", [128, 2, T], BF16)

    def v16(ap, cs):
        return ap.rearrange("(c p) t -> p c t", p=128)[:, :, cs]

    wout_v = wb_out.rearrange("(k p) n -> p k n", p=128)
    wg_v = wb_g.rearrange("(k p) n -> p k n", p=128)
    wu_v = wb_u.rearrange("(k p) n -> p k n", p=128)
    wpg_v = wb_pg.rearrange("(k p) n -> p k n", p=128)
    wpp_v = wb_pp.rearrange("(k p) n -> p k n", p=128)
    pT_v = pT.rearrange("(c p) t -> p c t", p=128)

    def residual_update(gname, ssbank):
        r = rstd_from(ssbank, float(D))
        go = SM[gname]
        for c in range(16):
            k = nxt("tmpf", 2)
            R.op("dve", lambda e, c=c, k=k: e.scalar_tensor_tensor(
                out=tmpf[k][:], in0=B[:, c, :], scalar=sm[:, go + c:go + c + 1], in1=rstd[r][:],
                op0=ALU.mult, op1=ALU.mult), reads=[("B", c), ("rstd", r)], writes=[("tmpf", k)])
            R.op("dve", lambda e, c=c, k=k: e.tensor_tensor(out=H[:, c, :], in0=H[:, c, :], in1=tmpf[k][:],
                                                            op=ALU.add),
                 reads=[("H", c), ("tmpf", k)], writes=[("H", c)])

    def norm_to_bf16(src, srckey, dst, dstkey, gname, ssbank, c0, n, dim):
        rms_accum(lambda c: src[:, c0 + c, :], n, lambda c: (srckey, c0 + c), ssbank)
        r = rstd_from(ssbank, float(dim))
        go = SM[gname]
        for c in range(n):
            R.op("dve", lambda e, c=c: e.scalar_tensor_tensor(
                out=dst[:, c0 + c, :], in0=src[:, c0 + c, :], scalar=sm[:, go + c:go + c + 1], in1=rstd[r][:],
                op0=ALU.mult, op1=ALU.mult), reads=[(srckey, c0 + c), ("rstd", r)], writes=[(dstkey, c0 + c)])

    for li in range(int(os.environ.get("K_NLD", str(NL)))):
        cs = slice(li * T, (li + 1) * T)
        R.op("sp", lambda e, cs=cs: e.dma_start(out=B[:, 0:8, :], in_=v16(at_s, cs)),
             writes=[("B", c) for c in range(8)], dma="ld_at")
        R.op("sp", lambda e, cs=cs: e.dma_start(out=B[:, 8:16, :], in_=v16(rec_s, cs)),
             writes=[("B", c) for c in range(8, 16)], dma="ld_rec")
        R.op("sp", lambda e, li=li: e.dma_start(out=H[:], in_=xT_v[:, :, 4096 + li * T:4096 + (li + 1) * T]),
             writes=[("H", c) for c in range(16)], dma="ld_x")
        R.op("sp", lambda e, cs=cs: e.dma_start(out=pf[:], in_=pT_v[:, :, cs]), writes=["pf"], dma="ld_p")
        DSTEP = int(os.environ.get("K_DSTEP", "9"))
        norm_to_bf16(B, "B", G, "G", "g_attn", 6, 0, 8, 1024)
        norm_to_bf16(B, "B", G, "G", "g_rec", 7, 8, 8, 1024)
        for g in range(4 if DSTEP >= 3 else 0):
            s, wv = wload(wout_v[:, :, g * 512:(g + 1) * 512], 16, 512, CW("c_out"))
            for m in range(4):
                ch = g * 4 + m
                p = ps_next()
                for kc in range(16):
                    R.op("pe", lambda e, kc=kc, m=m, p=p, wv=wv: e.matmul(
                        psb[p][:], lhsT=wv[:, kc, m * 128:(m + 1) * 128], rhs=G[:, kc, :],
                        start=(kc == 0), stop=(kc == 15)), reads=[("G", kc), ("w", s)], writes=[("ps", p)])
                R.op("dve", lambda e, ch=ch, p=p: e.tensor_copy(out=B[:, ch, :], in_=psb[p][:]),
                     reads=[("ps", p)], writes=[("B", ch)])
                sq_accum_one(B[:, ch, :], ("B", ch), 6, ch == 0, ch == 15)
        if DSTEP >= 3:
            residual_update("g_mix_post", 6)
        if DSTEP >= 4:
            norm_to_bf16(H, "H", Fb, "Fb", "g_ffn_pre", 7, 0, 16, D)
        for g in range(11 if DSTEP >= 5 else 0):
            sg_, wgv = wload(wg_v[:, :, g * 512:(g + 1) * 512], 16, 512, CW("c_g"))
            su_, wuv = wload(wu_v[:, :, g * 512:(g + 1) * 512], 16, 512, CW("c_u"))
            for m in range(4):
                f = g * 4 + m
                pg = ps_next()
                for kc in range(16):
                    R.op("pe", lambda e, kc=kc, m=m, pg=pg, wgv=wgv: e.matmul(
                        psb[pg][:], lhsT=wgv[:, kc, m * 128:(m + 1) * 128], rhs=Fb[:, kc, :],
                        start=(kc == 0), stop=(kc == 15)), reads=[("Fb", kc), ("w", sg_)],
                        writes=[("ps", pg)])
                pu = ps_next()
                for kc in range(16):
                    R.op("pe", lambda e, kc=kc, m=m, pu=pu, wuv=wuv: e.matmul(
                        psb[pu][:], lhsT=wuv[:, kc, m * 128:(m + 1) * 128], rhs=Fb[:, kc, :],
                        start=(kc == 0), stop=(kc == 15)), reads=[("Fb", kc), ("w", su_)],
                        writes=[("ps", pu)])
                k = nxt("sg", 2)
                R.op("act", lambda e, k=k, pg=pg: e.activation(out=sg[k][:], in_=psb[pg][:], func=AF.Sigmoid),
                     reads=[("ps", pg)], writes=[("sg", k)])
                R.op("dve", lambda e, k=k, pg=pg: e.tensor_tensor(out=sg[k][:], in0=sg[k][:], in1=psb[pg][:],
                                                                  op=ALU.mult),
                     reads=[("sg", k), ("ps", pg)], writes=[("sg", k)])
                R.op("dve", lambda e, k=k, pu=pu, f=f: e.tensor_tensor(out=G[:, f, :], in0=sg[k][:],
                                                                       in1=psb[pu][:], op=ALU.mult),
                     reads=[("sg", k), ("ps", pu)], writes=[("G", f)])
        for m in range(16 if DSTEP >= 6 else 0):
            s, wv = wload(wb_d[m].rearrange("p (f n) -> p f n", n=128), NFF, 128, CW("c_d"))
            p = ps_next()
            for f in range(NFF):
                R.op("pe", lambda e, f=f, p=p, wv=wv: e.matmul(
                    psb[p][:], lhsT=wv[:, f, :], rhs=G[:, f, :], start=(f == 0), stop=(f == NFF - 1)),
                    reads=[("G", f), ("w", s)], writes=[("ps", p)])
            R.op("dve", lambda e, m=m, p=p: e.tensor_copy(out=B[:, m, :], in_=psb[p][:]),
                 reads=[("ps", p)], writes=[("B", m)])
            sq_accum_one(B[:, m, :], ("B", m), 6, m == 0, m == 15)
        if DSTEP >= 6:
            residual_update("g_ffn_post", 6)
        if DSTEP < 7:
            for c in range(16):
                R.op("sp", lambda e, cs=cs, c=c: e.dma_start(out=outT[c * 128:(c + 1) * 128, cs],
                                                             in_=H[:, c, :]),
                     reads=[("H", c)], writes=[("out", li, c)], dma=("st_out", c))
            continue
        norm_to_bf16(H, "H", G, "G", "g_ple_pre", 7, 0, 16, D)
        R.op("act", lambda e: e.activation(out=pb[:, 0, :], in_=pf[:, 0, :], func=AF.Copy),
             reads=["pf"], writes=["pb"])
        R.op("act", lambda e: e.activation(out=pb[:, 1, :], in_=pf[:, 1, :], func=AF.Copy),
             reads=["pf"], writes=["pb"])
        for g in range(4):
            spp, wppv = wload(wpp_v[:, :, g * 512:(g + 1) * 512], 2, 512, CW("c_pp"))
            s, wv = wload(wpg_v[:, :, g * 512:(g + 1) * 512], 16, 512, CW("c_pg"))
            for m in range(4):
                ch = g * 4 + m
                pg = ps_next()
                for kc in range(16):
                    R.op("pe", lambda e, kc=kc, m=m, pg=pg, wv=wv: e.matmul(
                        psb[pg][:], lhsT=wv[:, kc, m * 128:(m + 1) * 128], rhs=G[:, kc, :],
                        start=(kc == 0), stop=(kc == 15)), reads=[("G", kc), ("w", s)], writes=[("ps", pg)])
                pp = ps_next()
                for kc in range(2):
                    R.op("pe", lambda e, kc=kc, m=m, pp=pp, wppv=wppv: e.matmul(
                        psb[pp][:], lhsT=wppv[:, kc, m * 128:(m + 1) * 128], rhs=pb[:, kc, :],
                        start=(kc == 0), stop=(kc == 1)), reads=["pb", ("w", spp)], writes=[("ps", pp)])
                k = nxt("sg", 2)
                R.op("act", lambda e, k=k, pg=pg: e.activation(out=sg[k][:], in_=psb[pg][:], func=AF.Sigmoid),
                     reads=[("ps", pg)], writes=[("sg", k)])
                R.op("dve", lambda e, k=k, pp=pp, ch=ch: e.tensor_tensor(out=B[:, ch, :], in0=sg[k][:],
                                                                         in1=psb[pp][:], op=ALU.mult),
                     reads=[("sg", k), ("ps", pp)], writes=[("B", ch)])
                sq_accum_one(B[:, ch, :], ("B", ch), 6, ch == 0, ch == 15)
        residual_update("g_ple_post", 6)
        if os.environ.get("K_SAN"):
            for c in range(16):
                R.op("dve", lambda e, c=c: e.tensor_scalar(out=H[:, c, :], in0=H[:, c, :], scalar1=-1e30,
                                                           scalar2=1e30, op0=ALU.max, op1=ALU.min),
                     reads=[("H", c)], writes=[("H", c)])
        for c in range(16):
            R.op("sp", lambda e, cs=cs, c=c: e.dma_start(out=outT[c * 128:(c + 1) * 128, cs], in_=H[:, c, :]),
                 reads=[("H", c)], writes=[("out", li, c)], dma=("st_out", c))

    R.barrier()
    R.emit(es)
    es.close()
    return nc


def _chunkcols(v, n):
    return np.ascontiguousarray(np.asarray(v, np.float32).reshape(n, 128).T)


def _bias_tables(rpb, half):
    G = np.zeros((8, 128, 6, 320), np.float32)
    M = np.full((128, 6, 320), -1e9, np.float32)
    reps = [30, 31, 60, 61, 62, 63]
    cl = np.arange(64)
    for cls, j in enumerate(reps):
        if j < 60:
            tb = j - 4 - (j % 2)
            ntl = 5
        else:
            tb = 56
            ntl = 4
        for t in range(ntl):
            for rr in range(2):
                jk = tb + 2 * t + rr
                if jk > 63:
                    continue
                if half == 1:
                    r, rk = 64 + j, 64 + jk
                    c, kc = cl, cl
                else:
                    r, rk = 63 - j, 63 - jk
                    c, kc = 63 - cl, 63 - cl
                rstart = min(max(r - 4, 0), 120)
                if not (rstart <= rk < rstart + 8):
                    continue
                dr = rk - r + 7
                cstart = np.clip(c - 8, 0, 48)
                ok = (kc[:, None] >= cstart[None, :]) & (kc[:, None] < cstart[None, :] + 16)
                dc = np.clip(kc[:, None] - c[None, :] + 15, 0, 30)
                vals = rpb[:, dr, :][:, dc]
                ks = slice(rr * 64, rr * 64 + 64)
                qs = slice(t * 64, t * 64 + 64)
                G[:, ks, cls, qs] = np.where(ok[None], vals, 0.0)
                M[ks, cls, qs] = np.where(ok, 0.0, -1e9)
    return G.reshape(8, 128, 6 * 320), M.reshape(128, 6 * 320)


_NC_CACHE = {}


def make_in_maps(x, p, g_mix_pre, w_in, rpb, conv_w, conv_b, w_rg_a, b_rg_a, w_rg_i, b_rg_i, lam, g_attn_out,
           g_rec_out, w_out, g_mix_post, g_ffn_pre, w_ffn_gate, w_ffn_up, w_ffn_down, g_ffn_post,
           g_ple_pre, w_ple_gate, w_ple_proj, g_ple_post):
    f = lambda a: np.asarray(a, np.float32)
    x, p = f(x), f(p)
    shared = {
        "w_in": np.ascontiguousarray(f(w_in)[0]), "w_out": np.ascontiguousarray(f(w_out)[0]),
        "w_g": np.ascontiguousarray(f(w_ffn_gate)[0]), "w_u": np.ascontiguousarray(f(w_ffn_up)[0]),
        "w_d": np.ascontiguousarray(f(w_ffn_down)[0]), "w_pg": np.ascontiguousarray(f(w_ple_gate)[0]),
        "w_pp": np.ascontiguousarray(f(w_ple_proj)[0]), "ident": np.eye(128, dtype=np.float32),
    }
    cw = f(conv_w)[0]
    z1024 = np.zeros((1024,), np.float32)
    in_maps = []
    for c in range(8):
        b, half = c // 2, c % 2
        if half == 1:
            seq = x[b]
            pseq = p[0, b]
            dirs = [0, 1]
            cw5 = np.stack([cw[0], cw[1], cw[2], cw[3], z1024], 0)
        else:
            seq = x[b][::-1]
            pseq = p[0, b][::-1]
            dirs = [1, 0]
            cw5 = np.stack([z1024, cw[3], cw[2], cw[1], cw[0]], 0)
        smv = np.zeros((128, NS), np.float32)

        def put(name, arr):
            smv[:, SM[name]:SM[name] + arr.shape[1]] = arr
        put("g_mix_pre", _chunkcols(f(g_mix_pre)[0], 16))
        put("g_attn", _chunkcols(f(g_attn_out)[0], 8))
        put("g_rec", _chunkcols(f(g_rec_out)[0], 8))
        put("g_mix_post", _chunkcols(f(g_mix_post)[0], 16))
        put("g_ffn_pre", _chunkcols(f(g_ffn_pre)[0], 16))
        put("g_ffn_post", _chunkcols(f(g_ffn_post)[0], 16))
        put("g_ple_pre", _chunkcols(f(g_ple_pre)[0], 16))
        put("g_ple_post", _chunkcols(f(g_ple_post)[0], 16))
        put("conv_b", _chunkcols(f(conv_b)[0], 8))
        put("cw5", np.ascontiguousarray(cw5.reshape(5, 8, 128).transpose(2, 1, 0).reshape(128, 40)))
        for nm, arr in [("b_a", f(b_rg_a)[0]), ("b_i", f(b_rg_i)[0]), ("lam", f(lam)[0])]:
            a2 = arr[dirs]
            put(nm, np.ascontiguousarray(a2.reshape(2, 8, 128).transpose(2, 0, 1).reshape(128, 16)))
        wa = np.ascontiguousarray(f(w_rg_a)[0][dirs].transpose(2, 0, 1, 3).reshape(128, 2 * 8 * 128))
        wi = np.ascontiguousarray(f(w_rg_i)[0][dirs].transpose(2, 0, 1, 3).reshape(128, 2 * 8 * 128))
        bg, bm = _bias_tables(f(rpb)[0], half)
        m = dict(shared)
        m.update({
            "xT": np.ascontiguousarray(seq.T), "pT": np.ascontiguousarray(pseq[4096:].T),
            "sm": smv, "wa": wa, "wi": wi, "biasG": bg, "biasM": bm,
        })
        in_maps.append(m)
    return in_maps


def kernel(x, p, g_mix_pre, w_in, rpb, conv_w, conv_b, w_rg_a, b_rg_a, w_rg_i, b_rg_i, lam, g_attn_out,
           g_rec_out, w_out, g_mix_post, g_ffn_pre, w_ffn_gate, w_ffn_up, w_ffn_down, g_ffn_post,
           g_ple_pre, w_ple_gate, w_ple_proj, g_ple_post):
    in_maps = make_in_maps(x, p, g_mix_pre, w_in, rpb, conv_w, conv_b, w_rg_a, b_rg_a, w_rg_i, b_rg_i, lam,
                           g_attn_out, g_rec_out, w_out, g_mix_post, g_ffn_pre, w_ffn_gate, w_ffn_up,
                           w_ffn_down, g_ffn_post, g_ple_pre, w_ple_gate, w_ple_proj, g_ple_post)
    if "nc" not in _NC_CACHE:
        _NC_CACHE["nc"] = build_nc()
    nc = _NC_CACHE["nc"]
    res = run_bass_kernel_spmd(nc, in_maps, core_ids=list(range(8)))
    out = np.empty((4, 8192, D), np.float32)
    for c in range(8):
        b, half = c // 2, c % 2
        o = np.asarray(res.results[c]["outT"], np.float32).T
        if half == 1:
            out[b, 4096:] = o
        else:
            out[b, :4096] = o[::-1]
    return out
```
